# Optimizing a Trainium2 kernel written in Bass

```python
import jax, jax.numpy as jnp
from jax import lax
import numpy as np

D_MODEL = 4096
BATCH = 1
SEQ = 16384
DEPTH = 4

POOL_GROUPS = 4
POOL_WIDTH = D_MODEL // 4
POOL_GROUP_DIM = POOL_WIDTH // POOL_GROUPS
POOL_WINDOWS = (2, 4, 8, 16)
N_HEADS = 16
HEAD_DIM = 64
ATTN_WIDTH = N_HEADS * HEAD_DIM
GRID_W = 64
WIN_ROWS_MAX = 8
WIN_COLS = 16
IN_WIDTH = POOL_WIDTH + 3 * ATTN_WIDTH
N_BRANCHES = 2
D_FF = 4 * D_MODEL
PLE_DIM = 256
PLE_GATE_RANK = 256
EPS = 1e-6

kernel_name = 'hybrid_pool_natten_encoder'


def _rmsnorm(x, g):
    xf = x.astype(jnp.float32)
    y = xf * lax.rsqrt(jnp.mean(xf * xf, axis=-1, keepdims=True) + EPS)
    return (y * g.astype(jnp.float32)).astype(x.dtype)


def _pool_mixer(u, w_pool, pool_scale):
    B, S, _ = u.shape
    t = jnp.arange(S)
    uf = u.astype(jnp.float32)
    csum = jnp.concatenate([jnp.zeros((B, 1, POOL_WIDTH), jnp.float32),
                            jnp.cumsum(uf, axis=1)], axis=1)
    outs = []
    for g, w in enumerate(POOL_WINDOWS):
        lo_c, hi_c = g * POOL_GROUP_DIM, (g + 1) * POOL_GROUP_DIM
        lo = jnp.clip(t - w // 2, 0, S)
        hi = jnp.clip(t + w // 2, 0, S)
        cg = csum[:, :, lo_c:hi_c]
        cnt = (hi - lo).astype(jnp.float32)[None, :, None]
        mean = (jnp.take(cg, hi, axis=1) - jnp.take(cg, lo, axis=1)) / cnt
        d = (mean - uf[:, :, lo_c:hi_c]).astype(u.dtype)
        outs.append(d @ w_pool[g])
    return jnp.concatenate(outs, axis=-1) * pool_scale


def _neighbourhood_attention(q, k, v, q_norm, k_norm, rpb):
    B, S, _ = q.shape
    rows = S // GRID_W
    kr = min(WIN_ROWS_MAX, rows)
    grid = (B, rows, GRID_W, N_HEADS, HEAD_DIM)
    qg = _rmsnorm(q.reshape(grid), q_norm)
    kg = _rmsnorm(k.reshape(grid), k_norm)
    vg = v.reshape(grid)
    cols = jnp.arange(GRID_W)
    c0 = jnp.clip(cols - WIN_COLS // 2, 0, GRID_W - WIN_COLS)
    col_idx = c0[:, None] + jnp.arange(WIN_COLS)[None, :]
    dc_idx = col_idx - cols[:, None] + (WIN_COLS - 1)
    rpb_c = rpb[:, :, dc_idx]
    scale = HEAD_DIM ** -0.5

    def row_block(r):
        r0 = jnp.clip(r - kr // 2, 0, rows - kr)
        q_r = lax.dynamic_index_in_dim(qg, r, axis=1, keepdims=False)
        k_r = lax.dynamic_slice_in_dim(kg, r0, kr, axis=1)
        v_r = lax.dynamic_slice_in_dim(vg, r0, kr, axis=1)
        k_w = k_r[:, :, col_idx]
        v_w = v_r[:, :, col_idx]
        s = jnp.einsum('bchd,bicjhd->bhcij', q_r, k_w,
                       preferred_element_type=jnp.float32) * scale
        dr_idx = r0 + jnp.arange(kr) - r + (WIN_ROWS_MAX - 1)
        bias = jnp.transpose(rpb_c[:, dr_idx], (0, 2, 1, 3))
        s = s + bias[None].astype(jnp.float32)
        pr = jax.nn.softmax(s.reshape(B, N_HEADS, GRID_W, kr * WIN_COLS), axis=-1)
        pr = pr.reshape(s.shape).astype(v.dtype)
        return jnp.einsum('bhcij,bicjhd->bchd', pr, v_w)

    o = lax.map(row_block, jnp.arange(rows))
    return jnp.transpose(o, (1, 0, 2, 3, 4)).reshape(B, S, ATTN_WIDTH)


def setup_inputs(seed: int = 0) -> dict:
    key = jax.random.key(seed)
    ks = jax.random.split(key, 20)
    L, D = DEPTH, D_MODEL

    def w(k, shape, fan_in):
        return jax.random.normal(k, shape, jnp.float32) * (fan_in ** -0.5)

    def gain(k, shape):
        return 1.0 + 0.1 * jax.random.normal(k, shape, jnp.float32)

    return {
        'x': jax.random.normal(ks[0], (BATCH, SEQ, D), jnp.float32),
        'p': jax.random.normal(ks[1], (DEPTH, BATCH, SEQ, PLE_DIM), jnp.float32),
        'norm_mix': gain(ks[2], (L, D)),
        'w_in': w(ks[3], (L, D, IN_WIDTH), D),
        'w_pool': w(ks[4], (L, POOL_GROUPS, POOL_GROUP_DIM, POOL_GROUP_DIM), POOL_GROUP_DIM),
        'pool_scale': gain(ks[5], (L, POOL_WIDTH)),
        'q_norm': gain(ks[6], (L, HEAD_DIM)),
        'k_norm': gain(ks[7], (L, HEAD_DIM)),
        'rpb': 0.1 * jax.random.normal(ks[8], (L, N_HEADS, 2 * WIN_ROWS_MAX - 1, 2 * WIN_COLS - 1), jnp.float32),
        'w_branch_pool': w(ks[9], (L, POOL_WIDTH, D), POOL_WIDTH),
        'w_branch_attn': w(ks[10], (L, ATTN_WIDTH, D), ATTN_WIDTH),
        'w_gate': w(ks[11], (L, D, N_BRANCHES * D), D),
        'w_out': w(ks[12], (L, D, D), D),
        'norm_mlp': gain(ks[13], (L, D)),
        'w_up': w(ks[14], (L, D, D_FF), D),
        'w_down': w(ks[15], (L, D_FF, D), D_FF),
        'norm_ple': gain(ks[16], (L, D)),
        'w_ple_gate_down': w(ks[17], (L, D, PLE_GATE_RANK), D),
        'w_ple_gate_up': w(ks[18], (L, PLE_GATE_RANK, D), PLE_GATE_RANK),
        'w_ple_proj': w(ks[19], (L, PLE_DIM, D), PLE_DIM),
    }


def reference(x, p, norm_mix, w_in, w_pool, pool_scale, q_norm, k_norm, rpb,
              w_branch_pool, w_branch_attn, w_gate, w_out, norm_mlp, w_up, w_down,
              norm_ple, w_ple_gate_down, w_ple_gate_up, w_ple_proj):
    for i in range(DEPTH):
        h = _rmsnorm(x, norm_mix[i])
        z = h @ w_in[i]
        u = z[..., :POOL_WIDTH]
        q = z[..., POOL_WIDTH:POOL_WIDTH + ATTN_WIDTH]
        k = z[..., POOL_WIDTH + ATTN_WIDTH:POOL_WIDTH + 2 * ATTN_WIDTH]
        v = z[..., POOL_WIDTH + 2 * ATTN_WIDTH:]
        a = _pool_mixer(u, w_pool[i], pool_scale[i]) @ w_branch_pool[i]
        b = _neighbourhood_attention(q, k, v, q_norm[i], k_norm[i], rpb[i]) @ w_branch_attn[i]
        gates = jax.nn.sigmoid(h @ w_gate[i])
        merged = gates[..., :D_MODEL] * a + gates[..., D_MODEL:] * b
        x = x + merged @ w_out[i]
        h = _rmsnorm(x, norm_mlp[i])
        x = x + jnp.square(jax.nn.relu(h @ w_up[i])) @ w_down[i]
        h = _rmsnorm(x, norm_ple[i])
        g = jax.nn.sigmoid((h @ w_ple_gate_down[i]) @ w_ple_gate_up[i])
        x = x + g * (p[i] @ w_ple_proj[i])
    return x
```

```python
import numpy as np
import concourse.bass as bass
import concourse.mybir as mybir
from concourse.bass_utils import run_bass_kernel_spmd

F32 = mybir.dt.float32
BF16 = mybir.dt.bfloat16
AF = mybir.ActivationFunctionType
ALU = mybir.AluOpType

NEG = -30000.0
EPS = 1e-6
GW = 64
TT = 512


class Cfg:
    def __init__(self, D=4096, L=4, NC=4, NR=256, DFF=None, PLE=256, RANK=256):
        self.D = D; self.L = L; self.NC = NC; self.NR = NR
        self.RO = NR // NC
        assert self.RO % 8 == 0 and NR >= 8
        self.PW = D // 4; self.AW = D // 4
        self.NH = self.AW // 64
        self.KC = D // 128
        self.PC = self.PW // 128
        self.AC = self.AW // 128
        self.GC = self.PC // 4
        assert self.GC >= 1
        self.DFF = DFF or 4 * D
        self.FC = self.DFF // 128
        self.SLAB = 16
        self.NSLAB = self.FC // self.SLAB
        self.PLE = PLE; self.RANK = RANK
        assert PLE == 256 and RANK == 256
        self.HALO = 4 * L
        self.ROWS0 = self.RO + 2 * self.HALO
        self.T0 = self.ROWS0 * GW
        self.NSLOT = 6
        self.WSZ = 32 * 128
        assert self.KC <= 32

    def K(self, l):
        return (4 * l, self.ROWS0 - 4 * l)

    def R(self, l):
        return (4 * (l + 1), self.ROWS0 - 4 * (l + 1))

    def pairs(self, rho):
        lo, hi = rho - 4, rho + 3
        H, RO = self.HALO, self.RO
        if H <= rho < H + 4:
            hi = max(hi, H + 7)
        if H + RO - 3 <= rho < H + RO:
            lo = min(lo, H + RO - 8)
        lo -= lo % 2
        return list(range(lo, hi + 1, 2))

    def special(self, rho):
        H, RO = self.HALO, self.RO
        return (H <= rho < H + 4) or (H + RO - 3 <= rho < H + RO)


def _tile_w(w, kk_per_tile=None):
    K, N = w.shape
    return np.ascontiguousarray(w.reshape(K // 128, 128, N // 128, 128).transpose(2, 1, 0, 3))


def _aux_layout(cfg):
    off = {}
    n = 0
    def add(name, width):
        nonlocal n
        off[name] = (n, width); n += width
    add("ident", 128); add("ones", 128); add("blk", 128)
    add("g", cfg.L * 3 * cfg.KC)
    add("qk", cfg.L * 2)
    add("ps", cfg.L * cfg.PC)
    add("rm", cfg.ROWS0 * 6)
    add("pc", 4 * 16)
    return off, n


def _build_bias_tables(cfg, rpb_l):
    NH = cfg.NH
    ck = np.arange(64)[:, None]; cq = np.arange(64)[None, :]
    c0 = np.clip(cq - 8, 0, 64 - 16)
    colvalid = (ck >= c0) & (ck < c0 + 16)
    dc = np.clip(ck - cq + 15, 0, 30)
    out = np.full((16, 128, NH, 64), NEG, np.float32)
    def fill(t, dr0, mask_first=False, mask_second=False):
        for half in range(2):
            if (half == 0 and mask_first) or (half == 1 and mask_second):
                continue
            dr = dr0 + half
            vals = rpb_l[:, dr + 7][:, dc]
            vals = np.where(colvalid[None], vals, np.float32(NEG))
            out[t, half * 64:(half + 1) * 64] = vals.transpose(1, 0, 2)
    for dr0 in range(-7, 7):
        fill(dr0 + 7, dr0)
    fill(14, -5, mask_first=True)
    fill(15, 3, mask_second=True)
    return out


def _host_prepare(cfg, inp):
    L, D, KC = cfg.L, cfg.D, cfg.KC
    f = lambda a: np.asarray(a, dtype=np.float32)
    shared = {}
    w_in = f(inp["w_in"]); w_gate = f(inp["w_gate"]); w_out = f(inp["w_out"])
    w_up = f(inp["w_up"]); w_down = f(inp["w_down"])
    w_bp = f(inp["w_branch_pool"]); w_ba = f(inp["w_branch_attn"])
    w_pd = f(inp["w_ple_gate_down"]); w_pu = f(inp["w_ple_gate_up"]); w_pp = f(inp["w_ple_proj"])
    w_pool = f(inp["w_pool"]); rpb = f(inp["rpb"])
    WS = cfg.WSZ
    def pad_tiles(t):
        n, p, x = t.shape
        if x == WS:
            return np.ascontiguousarray(t)
        o = np.zeros((n, p, WS), np.float32); o[:, :, :x] = t
        return o
    for l in range(L):
        tiles = {}
        tiles["win"] = _tile_w(w_in[l]).reshape(-1, 128, KC * 128)
        tiles["wg"] = _tile_w(w_gate[l]).reshape(-1, 128, KC * 128)
        tiles["wo"] = _tile_w(w_out[l]).reshape(-1, 128, KC * 128)
        tiles["wup"] = _tile_w(w_up[l]).reshape(-1, 128, KC * 128)
        wd = w_down[l].reshape(cfg.NSLAB, cfg.SLAB, 128, KC // 2, 2, 128)
        tiles["wdn"] = np.ascontiguousarray(wd.transpose(0, 3, 2, 4, 1, 5)).reshape(-1, 128, 2 * cfg.SLAB * 128)
        bp = w_bp[l].reshape(cfg.PC, 128, KC, 128)
        ba = w_ba[l].reshape(cfg.AC, 128, KC, 128)
        br = np.concatenate([bp.transpose(2, 1, 0, 3), ba.transpose(2, 1, 0, 3)], axis=2)
        tiles["wbr"] = np.ascontiguousarray(br).reshape(KC, 128, (cfg.PC + cfg.AC) * 128)
        tiles["wpd"] = _tile_w(w_pd[l]).reshape(-1, 128, KC * 128)
        pu = w_pu[l].reshape(2, 128, KC // 8, 8, 128)
        pp = w_pp[l].reshape(2, 128, KC // 8, 8, 128)
        pup = np.concatenate([pu.transpose(2, 1, 3, 0, 4), pp.transpose(2, 1, 3, 0, 4)], axis=3)
        tiles["wpu"] = np.ascontiguousarray(pup).reshape(KC // 8, 128, 8 * 4 * 128)
        GC = cfg.GC
        wp = w_pool[l].reshape(4, GC, 128, GC, 128)
        tiles["wpl"] = np.ascontiguousarray(wp.transpose(2, 0, 1, 3, 4)).reshape(1, 128, 4 * GC * GC * 128)
        for k, v in tiles.items():
            shared[f"{k}{l}"] = np.ascontiguousarray(v)
        shared[f"bt{l}"] = _build_bias_tables(cfg, rpb[l]).reshape(16, 128, cfg.NH * 64)
    off, naux = _aux_layout(cfg)
    x = f(inp["x"])[0]
    p = f(inp["p"])[:, 0]
    S = cfg.NR * GW
    cores = []
    for c in range(cfg.NC):
        base = c * cfg.RO - cfg.HALO
        aux = np.zeros((128, naux), np.float32)
        o, w = off["ident"]; aux[:, o:o + w] = np.eye(128, dtype=np.float32)
        o, w = off["ones"]; aux[:, o:o + w] = 1.0
        o, w = off["blk"]
        aux[0:64, o:o + 64] = 1.0; aux[64:128, o + 64:o + 128] = 1.0
        o, w = off["g"]
        for l in range(L):
            for j, nm in enumerate(("norm_mix", "norm_mlp", "norm_ple")):
                g = f(inp[nm])[l].reshape(KC, 128).T
                aux[:, o + (l * 3 + j) * KC: o + (l * 3 + j + 1) * KC] = g
        o, w = off["qk"]
        for l in range(L):
            aux[:, o + 2 * l] = np.tile(f(inp["q_norm"])[l], 2)
            aux[:, o + 2 * l + 1] = np.tile(f(inp["k_norm"])[l], 2)
        o, w = off["ps"]
        for l in range(L):
            aux[:, o + l * cfg.PC: o + (l + 1) * cfg.PC] = f(inp["pool_scale"])[l].reshape(cfg.PC, 128).T
        o, w = off["rm"]
        for rho in range(cfg.ROWS0):
            gq = rho + base
            if not (0 <= gq < cfg.NR):
                continue
            r0 = min(max(gq - 4, 0), cfg.NR - 8)
            for j, pr in enumerate(cfg.pairs(rho)):
                for half in range(2):
                    gk = pr + half + base
                    if not (r0 <= gk < r0 + 8):
                        aux[half * 64:(half + 1) * 64, o + rho * 6 + j] = NEG
        o, w = off["pc"]
        aux[:, o:o + w] = 1.0
        for g, wd_ in enumerate((2, 4, 8, 16)):
            for i in range(8):
                if c == 0:
                    t = i
                    lo = max(t - wd_ // 2, 0); hi = min(t + wd_ // 2, S)
                    aux[:, o + g * 16 + i] = np.float32(wd_) / np.float32(hi - lo)
                if c == cfg.NC - 1:
                    t = S - 8 + i
                    lo = max(t - wd_ // 2, 0); hi = min(t + wd_ // 2, S)
                    aux[:, o + g * 16 + 8 + i] = np.float32(wd_) / np.float32(hi - lo)
        rows = np.arange(cfg.ROWS0) + base
        ok = (rows >= 0) & (rows < cfg.NR)
        xT = np.zeros((D, cfg.T0), np.float32)
        pT = np.zeros((L, 256, cfg.T0), np.float32)
        r_lo = max(base, 0); r_hi = min(base + cfg.ROWS0, cfg.NR)
        t_lo = (r_lo - base) * GW; t_hi = (r_hi - base) * GW
        xT[:, t_lo:t_hi] = x[r_lo * GW:r_hi * GW].T
        for l in range(L):
            pT[l, :, t_lo:t_hi] = p[l, r_lo * GW:r_hi * GW].T
        valid = np.zeros((128, cfg.T0), np.float32)
        valid[:, t_lo:t_hi] = 1.0
        cores.append({"xT": xT, "pT": pT.reshape(L * 256, cfg.T0), "aux": aux, "valid": valid})
    return shared, cores


COMPUTE = ("pe", "act", "dve", "pool")
QUEUES = ("pe", "act", "dve", "pool", "sp")


class Op:
    __slots__ = ("eng", "fn", "deps", "needed", "seq", "is_dma", "dsem", "dval", "idx")


class Sched:
    def __init__(self):
        self.q = {e: [] for e in QUEUES}
        self.last_w = {}
        self.readers = {}
        self.dry = False
        self.bar = {e: set() for e in COMPUTE}
        self.recent_dma = []
        self.dcount = {}
        self.last_on = {}
        self.n = 0

    def add(self, eng, fn, reads=(), writes=(), dma=None, ring=False):
        if self.dry:
            return None
        op = Op()
        op.eng = eng; op.fn = fn; op.needed = False; op.seq = 0; op.idx = self.n; self.n += 1
        op.is_dma = dma is not None; op.dsem = dma; op.dval = 0
        deps = set()
        for r in reads:
            lw = self.last_w.get(r)
            if lw is not None:
                deps.add(lw)
        for w in writes:
            lw = self.last_w.get(w)
            if lw is not None:
                deps.add(lw)
            rd = self.readers.get(w)
            if rd:
                deps.update(rd[0].values()); deps.update(rd[1])
        if eng in self.bar and self.bar[eng]:
            deps |= self.bar[eng]; self.bar[eng] = set()
        op.deps = [d for d in deps if not (d.eng == "pe" and eng == "pe" and not d.is_dma and not op.is_dma)]
        if op.is_dma:
            self.dcount[dma] = self.dcount.get(dma, 0) + 16
            op.dval = self.dcount[dma]
            if not ring:
                self.recent_dma.append(op)
        for r in reads:
            rd = self.readers.setdefault(r, ({}, []))
            if op.is_dma:
                rd[1].append(op)
            else:
                rd[0][eng] = op
        for w in writes:
            self.last_w[w] = op
            self.readers[w] = ({}, [])
        self.q[eng].append(op)
        if not op.is_dma:
            self.last_on[eng] = op
        return op

    def barrier(self):
        if self.dry:
            return
        for e in COMPUTE:
            s = set(self.recent_dma)
            for e2 in COMPUTE:
                if e2 != e and e2 in self.last_on:
                    s.add(self.last_on[e2])
            self.bar[e] |= s
        self.recent_dma = []

    def finalize(self):
        for e in QUEUES:
            for op in self.q[e]:
                for d in op.deps:
                    d.needed = True
        for e in QUEUES:
            c = 0
            for op in self.q[e]:
                if op.needed and not op.is_dma:
                    c += 1; op.seq = c


def build_program(cfg):
    nc = bass.Bass("TRN2", target_bir_lowering=False)
    L, D, KC, PC, AC, NH, T0 = cfg.L, cfg.D, cfg.KC, cfg.PC, cfg.AC, cfg.NH, cfg.T0
    WS = cfg.WSZ
    off, naux = _aux_layout(cfg)

    xT_in = nc.dram_tensor("xT", [D, T0], F32, kind="ExternalInput").ap()
    pT_in = nc.dram_tensor("pT", [L * 256, T0], F32, kind="ExternalInput").ap()
    aux_in = nc.dram_tensor("aux", [128, naux], F32, kind="ExternalInput").ap()
    valid_in = nc.dram_tensor("valid", [128, T0], F32, kind="ExternalInput").ap()
    out = nc.dram_tensor("out", [D, cfg.RO * GW], F32, kind="ExternalOutput").ap()
    wnames = ["win", "wg", "wo", "wup", "wdn", "wbr", "wpd", "wpu", "wpl"]
    wcount = {"win": KC, "wg": 2 * KC, "wo": KC, "wup": cfg.FC, "wdn": cfg.NSLAB * (KC // 2),
              "wbr": KC, "wpd": 2, "wpu": KC // 8, "wpl": 1}
    wwidth = {"win": KC * 128, "wg": KC * 128, "wo": KC * 128, "wup": KC * 128, "wdn": 2 * cfg.SLAB * 128,
              "wbr": (PC + AC) * 128, "wpd": KC * 128, "wpu": 8 * 4 * 128, "wpl": 4 * cfg.GC * cfg.GC * 128}
    w32 = {}; w16 = {}
    for l in range(L):
        for nm in wnames:
            assert wwidth[nm] <= WS
            w32[(nm, l)] = nc.dram_tensor(f"{nm}{l}", [wcount[nm], 128, wwidth[nm]], F32, kind="ExternalInput").ap()
            w16[(nm, l)] = nc.dram_tensor(f"{nm}{l}_bf", [wcount[nm], 128, wwidth[nm]], BF16).ap()
        w32[("bt", l)] = nc.dram_tensor(f"bt{l}", [16, 128, NH * 64], F32, kind="ExternalInput").ap()
        w16[("bt", l)] = nc.dram_tensor(f"bt{l}_bf", [16, 128, NH * 64], BF16).ap()
    xs = nc.dram_tensor("xs", [D, T0], F32).ap()
    qs = nc.dram_tensor("qs", [cfg.AW, T0], BF16).ap()
    ks = nc.dram_tensor("ks", [cfg.AW, T0], BF16).ap()
    us = nc.dram_tensor("us", [cfg.PW, T0], F32).ap()
    vs = nc.dram_tensor("vs", [T0, cfg.AW], BF16).ap()
    ms = nc.dram_tensor("ms", [D, TT], BF16).ap()

    S = Sched()

    import contextlib
    with contextlib.ExitStack() as es:
        def sb(name, shape, dt):
            return es.enter_context(nc.sbuf_tensor(name, shape, dt))
        ring = sb("ring", [128, cfg.NSLOT, WS], BF16)
        auxs = sb("auxs", [128, naux], F32)
        identb = sb("identb", [128, 128], BF16)
        onesb = sb("onesb", [128, 128], BF16)
        blkb = sb("blkb", [128, 128], BF16)
        Rx = sb("Rx", [128, KC * TT], F32)
        Rh = sb("Rh", [128, KC * TT], BF16)
        Ra = sb("Ra", [128, max(cfg.SLAB, PC + AC) * TT], BF16)
        Rt = sb("Rt", [128, 6784], F32)
        psf = es.enter_context(nc.psum_tensor("psf", [128, 7, 512], F32))
        psb = es.enter_context(nc.psum_tensor("psb", [128, 1024], BF16))

        xt = Rx[:, :].rearrange("p (k t) -> p k t", t=TT)
        hb = Rh[:, :].rearrange("p (k t) -> p k t", t=TT)
        RxB = Rx[:, :].bitcast(BF16)
        o_ = 0
        kw = RxB[:, o_:o_ + AC * 1024].rearrange("p (c t) -> p c t", t=1024); o_ += AC * 1024
        vw = RxB[:, o_:o_ + 8 * cfg.AW].rearrange("p (j f) -> p j f", f=cfg.AW); o_ += 8 * cfg.AW
        qw = RxB[:, o_:o_ + AC * TT].rearrange("p (c t) -> p c t", t=TT); o_ += AC * TT
        assert o_ % 2 == 0
        uw = Rx[:, o_ // 2:o_ // 2 + PC * 528].rearrange("p (c t) -> p c t", t=528)
        assert o_ // 2 + PC * 528 <= KC * TT
        btw = Rh[:, 0:16 * NH * 64].rearrange("p (t h q) -> p t h q", t=16, h=NH)
        assert 16 * NH * 64 <= KC * TT
        aT = Ra[:, 0:cfg.SLAB * TT].rearrange("p (k t) -> p k t", t=TT)
        poolo = Ra[:, 0:PC * TT].rearrange("p (k t) -> p k t", t=TT)
        attno = Ra[:, PC * TT:(PC + AC) * TT].rearrange("p (k t) -> p k t", t=TT)
        RtB = Rt[:, :].bitcast(BF16)
        t_f = [Rt[:, i * 512:(i + 1) * 512] for i in range(4)]
        sqb = [RtB[:, 4096 + i * 512:4096 + (i + 1) * 512] for i in range(2)]
        stg = [RtB[:, 5120 + i * 512:5120 + (i + 1) * 512] for i in range(2)]
        rstd = Rt[:, 3072:3584]
        vtmp = Rt[:, 3584:4096]

        def A(name):
            o, w = off[name]
            return auxs[:, o:o + w]

        def gcol(l, j, kc):
            o, _ = off["g"]
            c = o + (l * 3 + j) * KC + kc
            return auxs[:, c:c + 1]

        class WQ:
            def __init__(self):
                self.reqs = []; self.i = 0; self.issued = 0

            def get(self):
                if S.dry:
                    self.reqs.append(self._cur)
                    return None
                i = self.i; self.i += 1
                assert self.reqs[i] == self._cur, (i, self.reqs[i], self._cur)
                while self.issued < min(len(self.reqs), i + cfg.NSLOT):
                    self._issue(self.issued); self.issued += 1
                return i % cfg.NSLOT

            def req(self, key, idx, n=None):
                self._cur = (key, idx, n if n is not None else wwidth[key[0]])
                return self.get()

            def _issue(self, j):
                key, idx, n = self.reqs[j]
                slot = j % cfg.NSLOT
                src = w16[key][idx, :, 0:n]
                dst = ring[:, slot, 0:n]
                S.add("sp", lambda e, d=dst, s=src, k=f"ring{slot}": e.dma_start(out=d, in_=s),
                      reads=[("wbf", key)], writes=[("ring", slot)], dma=f"ring{slot}", ring=True)

        wq = WQ()

        def wtile(slot, n=WS):
            return ring[:, slot, 0:n]

        def mm(out_ap, lhsT, rhs, start, stop, reads, writes):
            S.add("pe", lambda e: e.matmul(out_ap, lhsT=lhsT, rhs=rhs, start=start, stop=stop),
                  reads=reads, writes=writes)

        def act(out_ap, in_ap, func, reads, writes, bias=None, scale=None):
            kw_ = {}
            if bias is not None: kw_["bias"] = bias
            if scale is not None: kw_["scale"] = scale
            S.add("act", lambda e: e.activation(out=out_ap, in_=in_ap, func=func, **kw_), reads=reads, writes=writes)

        def tt(eng, out_ap, a, b, op, reads, writes):
            S.add(eng, lambda e: e.tensor_tensor(out=out_ap, in0=a, in1=b, op=op), reads=reads, writes=writes)

        def stt(out_ap, in0, scalar, in1, op0, op1, reads, writes):
            S.add("dve", lambda e: e.scalar_tensor_tensor(out=out_ap, in0=in0, scalar=scalar, in1=in1, op0=op0, op1=op1),
                  reads=reads, writes=writes)

        def dma(q, out_ap, in_ap, key, reads, writes):
            S.add(q, lambda e: e.dma_start(out=out_ap, in_=in_ap), reads=reads, writes=writes, dma=key)

        def blocks(prefix, r0, r1):
            return [(prefix, b) for b in range(r0 // 4, (r1 + 3) // 4)]

        PSB = lambda b: ("ps", b)

        def emit_norm(l, j, t0, use_valid):
            for kc in range(KC):
                b = sqb[kc % 2]
                act(b, xt[:, kc, :], AF.Square, reads=[("xt", kc)], writes=[("sqb", kc % 2)])
                mm(psf[:, 0, :], onesb[:, :], b, kc == 0, kc == KC - 1, reads=[("sqb", kc % 2)], writes=[PSB(0)])
            act(t_f[0], psf[:, 0, :], AF.Sqrt, reads=[PSB(0)], writes=[("tf", 0)], bias=epsc[:, 0:1], scale=1.0 / D)
            S.add("dve", lambda e: e.reciprocal(out=rstd, in_=t_f[0]), reads=[("tf", 0)], writes=[("rstd",)])
            if use_valid:
                dma("act", vtmp, valid_in[:, t0:t0 + TT], "vld", reads=[], writes=[("vtmp",)])
                tt("dve", rstd, rstd, vtmp, ALU.mult, reads=[("rstd",), ("vtmp",)], writes=[("rstd",)])
            for kc in range(KC):
                stt(hb[:, kc, :], xt[:, kc, :], gcol(l, j, kc), rstd, ALU.mult, ALU.mult,
                    reads=[("xt", kc), ("rstd",)], writes=[("hb", kc)])

        def proj(bank, slot, nk, rhs_fn, rhs_res, wview=None):
            wv = wview if wview is not None else wtile(slot, nk * 128).rearrange("p (k c) -> p k c", c=128)
            for k in range(nk):
                mm(psf[:, bank, :], wv[:, k, :], rhs_fn(k), k == 0, k == nk - 1,
                   reads=[("ring", slot), rhs_res(k)], writes=[PSB(bank)])

        def phase1(l, row0):
            t0 = row0 * GW
            src = xT_in if l == 0 else xs
            dma("act", xt, src[:, t0:t0 + TT].rearrange("(k p) t -> p k t", p=128), "xt",
                reads=blocks("xs", row0, row0 + 8), writes=[("xt", k) for k in range(KC)])
            emit_norm(l, 0, t0, True)
            qo, _ = off["qk"]
            noc = PC + 3 * AC
            for oc in range(noc):
                slot = wq.req(("win", l), oc)
                bank = 1 + (oc % 2)
                if not S.dry:
                    proj(bank, slot, KC, lambda k: hb[:, k, :], lambda k: ("hb", k))
                    P = psf[:, bank, :]
                    if oc < PC:
                        i = oc % 2
                        act(t_f[1 + i], P, AF.Copy, reads=[PSB(bank)], writes=[("tf", 1 + i)])
                        dma("act", us[oc * 128:(oc + 1) * 128, t0:t0 + TT], t_f[1 + i], f"ust{i}",
                            reads=[("tf", 1 + i)], writes=blocks("us", row0, row0 + 8))
                    elif oc < PC + 2 * AC:
                        isq = oc < PC + AC
                        c = (oc - PC) % AC
                        i = oc % 2
                        act(sqb[i], P, AF.Square, reads=[PSB(bank)], writes=[("sqb", i)])
                        mm(psf[:, 5, :], blkb[:, :], sqb[i], True, True, reads=[("sqb", i)], writes=[PSB(5)])
                        if isq:
                            act(t_f[3], psf[:, 5, :], AF.Sqrt, reads=[PSB(5)], writes=[("tf", 3)], bias=epsc[:, 1:2], scale=1.0)
                        else:
                            act(t_f[3], psf[:, 5, :], AF.Sqrt, reads=[PSB(5)], writes=[("tf", 3)], bias=epsc[:, 0:1], scale=1.0 / 64)
                        S.add("dve", lambda e: e.reciprocal(out=t_f[3], in_=t_f[3]), reads=[("tf", 3)], writes=[("tf", 3)])
                        gq = auxs[:, qo + 2 * l + (0 if isq else 1): qo + 2 * l + (0 if isq else 1) + 1]
                        stt(stg[i], P, gq, t_f[3], ALU.mult, ALU.mult, reads=[PSB(bank), ("tf", 3)], writes=[("stg", i)])
                        dst = (qs if isq else ks)[c * 128:(c + 1) * 128, t0:t0 + TT]
                        dma("act", dst, stg[i], f"qkst{i}", reads=[("stg", i)],
                            writes=blocks("qs" if isq else "ks", row0, row0 + 8))
                    else:
                        c = oc - PC - 2 * AC
                        i = oc % 2
                        act(stg[i], P, AF.Copy, reads=[PSB(bank)], writes=[("stg", i)])
                        for sj in range(4):
                            S.add("pe", lambda e, sj=sj, i=i: e.transpose(psb[:, sj * 128:(sj + 1) * 128], stg[i][:, sj * 128:(sj + 1) * 128], identb[:, :]),
                                  reads=[("stg", i)], writes=[("psb",)])
                        vst = Ra[:, 0:4 * cfg.AW].rearrange("p (s f) -> p s f", f=cfg.AW)
                        S.add("dve", lambda e, c=c: e.tensor_copy(out=vst[:, :, c * 128:(c + 1) * 128],
                                                                 in_=psb[:, 0:512].rearrange("p (s f) -> p s f", f=128)),
                              reads=[("psb",)], writes=[("vst",)])
                        if c == AC - 1:
                            dma("act", vs[t0:t0 + TT, :].rearrange("(s p) f -> p s f", p=128), vst, "vst",
                                reads=[("vst",)], writes=blocks("vs", row0, row0 + 8))

        def phaseA(l, row0):
            t0 = row0 * GW
            kr0 = row0 - 4
            dma("act", kw, ks[:, kr0 * GW:kr0 * GW + 1024].rearrange("(c p) t -> p c t", p=128), "kw",
                reads=blocks("ks", kr0, kr0 + 16), writes=[("kw",)])
            dma("act", vw, vs[kr0 * GW:kr0 * GW + 1024, :].rearrange("(j p) f -> p j f", p=128), "vw",
                reads=blocks("vs", kr0, kr0 + 16), writes=[("vw",)])
            dma("act", qw, qs[:, t0:t0 + TT].rearrange("(c p) t -> p c t", p=128), "qw",
                reads=blocks("qs", row0, row0 + 8), writes=[("qw",)])
            dma("act", uw, us[:, t0 - 8:t0 + TT + 8].rearrange("(c p) t -> p c t", p=128), "uw",
                reads=blocks("us", row0 - 1, row0 + 9), writes=[("uw",)])
            dma("act", btw, w16[("bt", l)].rearrange("t p (h q) -> p t h q", q=64), "btw",
                reads=[("wbf", ("bt", l))], writes=[("btw",)])
            rmo, _ = off["rm"]
            pT = [RtB[:, 8192 + i * 512: 8192 + i * 512 + 384] for i in range(2)]
            sbank = [1, 2]
            n_sc = 0
            for hc in range(AC):
                ob, db = 3 + (hc % 2) * 2, 4 + (hc % 2) * 2
                for rr in range(8):
                    rho = row0 + rr
                    prs = cfg.pairs(rho)
                    spec = cfg.special(rho)
                    for hh in range(2):
                        h = hc * 2 + hh
                        hp = slice(hh * 64, hh * 64 + 64)
                        pi_ = n_sc % 2; sbk = sbank[pi_]; pt = pT[pi_]; n_sc += 1
                        for j, pr in enumerate(prs):
                            kt = (pr - kr0) * GW
                            dr0 = pr - rho
                            if spec:
                                tab = dr0 + 7
                            else:
                                tab = 14 if (j == 0 and dr0 == -5) else (15 if (dr0 == 3) else dr0 + 7)
                            sc = psf[:, sbk, j * 64:(j + 1) * 64]
                            mm(sc, kw[hp, hc, kt:kt + 128], qw[hp, hc, rr * 64:(rr + 1) * 64], True, False,
                               reads=[("kw",), ("qw",)], writes=[PSB(sbk)])
                            mm(sc, identb[:, :], btw[:, tab, h, :], False, True,
                               reads=[("btw",)], writes=[PSB(sbk)])
                        if spec:
                            for j in range(len(prs)):
                                col = rmo + rho * 6 + j
                                act(pt[:, j * 64:(j + 1) * 64], psf[:, sbk, j * 64:(j + 1) * 64], AF.Exp,
                                    reads=[PSB(sbk)], writes=[("pt", pi_)], bias=auxs[:, col:col + 1], scale=1.0)
                        else:
                            n = len(prs) * 64
                            act(pt[:, 0:n], psf[:, sbk, 0:n], AF.Exp, reads=[PSB(sbk)], writes=[("pt", pi_)])
                        for j, pr in enumerate(prs):
                            vj = (pr - kr0) // 2
                            mm(psf[hp, ob, rr * 64:(rr + 1) * 64], vw[:, vj, h * 64:(h + 1) * 64], pt[:, j * 64:(j + 1) * 64],
                               j == 0, j == len(prs) - 1, reads=[("vw",), ("pt", pi_)], writes=[PSB(ob)])
                        for j, pr in enumerate(prs):
                            mm(psf[hp, db, rr * 64:(rr + 1) * 64], onesb[:, 0:64], pt[:, j * 64:(j + 1) * 64],
                               j == 0, j == len(prs) - 1, reads=[("pt", pi_)], writes=[PSB(db)])
                i = hc % 2
                S.add("dve", lambda e, db=db, i=i: e.reciprocal(out=t_f[i], in_=psf[:, db, :]), reads=[PSB(db)], writes=[("tf", i)])
                tt("dve", attno[:, hc, :], psf[:, ob, :], t_f[i], ALU.mult, reads=[PSB(ob), ("tf", i)], writes=[("ra", PC + hc)])
            slot = wq.req(("wpl", l), 0, 4 * cfg.GC * cfg.GC * 128)
            if S.dry:
                return
            GCn = cfg.GC
            wpv = wtile(slot, 4 * GCn * GCn * 128).rearrange("p (g k o c) -> p g k o c", g=4, k=GCn, o=GCn)
            pco, _ = off["pc"]
            pso, _ = off["ps"]
            sA = Rt[:, 4608:4608 + GCn * 528].rearrange("p (c t) -> p c t", t=528)
            sB = Rt[:, 5664:5664 + GCn * 528].rearrange("p (c t) -> p c t", t=528)
            dT = RtB[:, 2048:2048 + GCn * TT].rearrange("p (c t) -> p c t", t=TT)
            H0 = cfg.HALO
            for g, wd_ in enumerate((2, 4, 8, 16)):
                U = uw[:, g * GCn:(g + 1) * GCn, :]
                kst = wd_.bit_length() - 1
                cur, cur_res = U, ("uw",)
                n = 528
                bufs = [sA, sB]
                for st_ in range(kst):
                    sh = 1 << st_
                    n2 = n - sh
                    dst = bufs[st_ % 2]
                    tt("dve", dst[:, :, 0:n2], cur[:, :, 0:n2], cur[:, :, sh:sh + n2], ALU.add,
                       reads=[cur_res], writes=[("pl", st_ % 2)])
                    cur, cur_res, n = dst, ("pl", st_ % 2), n2
                o0 = 8 - wd_ // 2
                for (rb, side) in ((H0, 0), (H0 + cfg.RO - 1, 1)):
                    if row0 <= rb < row0 + 8:
                        tl = (rb - row0) * GW + (0 if side == 0 else GW - 8)
                        cc = auxs[:, pco + g * 16 + side * 8: pco + g * 16 + side * 8 + 8]
                        for c_ in range(GCn):
                            tt("dve", cur[:, c_, o0 + tl:o0 + tl + 8], cur[:, c_, o0 + tl:o0 + tl + 8], cc, ALU.mult,
                               reads=[cur_res], writes=[cur_res])
                for c_ in range(GCn):
                    stt(dT[:, c_, :], cur[:, c_, o0:o0 + TT], 1.0 / wd_, U[:, c_, 8:8 + TT], ALU.mult, ALU.subtract,
                        reads=[cur_res, ("uw",)], writes=[("tf", 2)])
                for oc in range(GCn):
                    bank = 1 + (oc + g) % 2
                    for k in range(GCn):
                        mm(psf[:, bank, :], wpv[:, g, k, oc, :], dT[:, k, :], k == 0, k == GCn - 1,
                           reads=[("ring", slot), ("tf", 2)], writes=[PSB(bank)])
                    ch = g * GCn + oc
                    S.add("dve", lambda e, ch=ch, bank=bank: e.tensor_scalar(out=poolo[:, ch, :], in0=psf[:, bank, :],
                                                                             scalar1=auxs[:, pso + l * PC + ch: pso + l * PC + ch + 1],
                                                                             scalar2=None, op0=ALU.mult),
                          reads=[PSB(bank)], writes=[("ra", ch)])

        def phaseB(l, row0):
            t0 = row0 * GW
            src = xT_in if l == 0 else xs
            dma("act", xt, src[:, t0:t0 + TT].rearrange("(k p) t -> p k t", p=128), "xt",
                reads=blocks("xs", row0, row0 + 8), writes=[("xt", k) for k in range(KC)])
            emit_norm(l, 0, t0, False)
            nbr = PC + AC
            for dc in range(KC):
                bA, bB = 1 + (dc % 2), 3 + (dc % 2)
                sgA = wq.req(("wg", l), dc)
                if not S.dry:
                    proj(bA, sgA, KC, lambda k: hb[:, k, :], lambda k: ("hb", k))
                sgB = wq.req(("wg", l), KC + dc)
                if not S.dry:
                    proj(bB, sgB, KC, lambda k: hb[:, k, :], lambda k: ("hb", k))
                sbr = wq.req(("wbr", l), dc, nbr * 128)
                if S.dry:
                    continue
                bv = wtile(sbr, nbr * 128).rearrange("p (k c) -> p k c", c=128)
                for k in range(PC):
                    mm(psf[:, 5, :], bv[:, k, :], poolo[:, k, :], k == 0, k == PC - 1,
                       reads=[("ring", sbr), ("ra", k)], writes=[PSB(5)])
                for k in range(AC):
                    mm(psf[:, 6, :], bv[:, PC + k, :], attno[:, k, :], k == 0, k == AC - 1,
                       reads=[("ring", sbr), ("ra", PC + k)], writes=[PSB(6)])
                act(t_f[0], psf[:, bA, :], AF.Sigmoid, reads=[PSB(bA)], writes=[("tf", 0)])
                act(t_f[1], psf[:, bB, :], AF.Sigmoid, reads=[PSB(bB)], writes=[("tf", 1)])
                tt("dve", t_f[0], t_f[0], psf[:, 5, :], ALU.mult, reads=[("tf", 0), PSB(5)], writes=[("tf", 0)])
                tt("dve", t_f[1], t_f[1], psf[:, 6, :], ALU.mult, reads=[("tf", 1), PSB(6)], writes=[("tf", 1)])
                i = dc % 2
                tt("dve", stg[i], t_f[0], t_f[1], ALU.add, reads=[("tf", 0), ("tf", 1)], writes=[("stg", i)])
                dma("act", ms[dc * 128:(dc + 1) * 128, :], stg[i], f"mst{i}", reads=[("stg", i)], writes=[("ms", dc)])
            dma("act", hb, ms.rearrange("(k p) t -> p k t", p=128), "hbld",
                reads=[("ms", k) for k in range(KC)], writes=[("hb", k) for k in range(KC)])
            for dc in range(KC):
                s = wq.req(("wo", l), dc)
                if S.dry:
                    continue
                bank = 1 + (dc % 2)
                proj(bank, s, KC, lambda k: hb[:, k, :], lambda k: ("hb", k))
                tt("dve", xt[:, dc, :], xt[:, dc, :], psf[:, bank, :], ALU.add, reads=[("xt", dc), PSB(bank)], writes=[("xt", dc)])
            emit_norm(l, 1, t0, False)
            for sl in range(cfg.NSLAB):
                for fcl in range(cfg.SLAB):
                    s = wq.req(("wup", l), sl * cfg.SLAB + fcl)
                    if S.dry:
                        continue
                    bank = 1 + (fcl % 2)
                    proj(bank, s, KC, lambda k: hb[:, k, :], lambda k: ("hb", k))
                    i = fcl % 2
                    act(t_f[i], psf[:, bank, :], AF.Relu, reads=[PSB(bank)], writes=[("tf", i)])
                    act(aT[:, fcl, :], t_f[i], AF.Square, reads=[("tf", i)], writes=[("ra", fcl)])
                for dcp in range(KC // 2):
                    s = wq.req(("wdn", l), sl * (KC // 2) + dcp, 2 * cfg.SLAB * 128)
                    if S.dry:
                        continue
                    dv = wtile(s, 2 * cfg.SLAB * 128).rearrange("p (d k c) -> p d k c", d=2, k=cfg.SLAB)
                    for dcl in range(2):
                        dc = dcp * 2 + dcl
                        bank = 3 + (dc % 2)
                        for k in range(cfg.SLAB):
                            mm(psf[:, bank, :], dv[:, dcl, k, :], aT[:, k, :], k == 0, k == cfg.SLAB - 1,
                               reads=[("ring", s), ("ra", k)], writes=[PSB(bank)])
                        tt("dve", xt[:, dc, :], xt[:, dc, :], psf[:, bank, :], ALU.add,
                           reads=[("xt", dc), PSB(bank)], writes=[("xt", dc)])
            emit_norm(l, 2, t0, False)
            g1 = Ra[:, 0:2 * TT].rearrange("p (k t) -> p k t", t=TT)
            pb = Ra[:, 2 * TT:4 * TT].rearrange("p (k t) -> p k t", t=TT)
            pf = Rt[:, 1024:2048].rearrange("p (k t) -> p k t", t=TT)
            dma("act", pf, pT_in[l * 256:(l + 1) * 256, t0:t0 + TT].rearrange("(k p) t -> p k t", p=128), "pf",
                reads=[], writes=[("tf", 2), ("tf", 3)])
            S.add("dve", lambda e: e.tensor_copy(out=pb, in_=pf), reads=[("tf", 2), ("tf", 3)], writes=[("ra", 2), ("ra", 3)])
            for oc in range(2):
                s = wq.req(("wpd", l), oc)
                if S.dry:
                    continue
                bank = 1 + oc
                proj(bank, s, KC, lambda k: hb[:, k, :], lambda k: ("hb", k))
                act(g1[:, oc, :], psf[:, bank, :], AF.Copy, reads=[PSB(bank)], writes=[("ra", oc)])
            for grp in range(KC // 8):
                s = wq.req(("wpu", l), grp)
                if S.dry:
                    continue
                pv = wtile(s, 8 * 4 * 128).rearrange("p (d k c) -> p d k c", d=8, k=4)
                for dcl in range(8):
                    dc = grp * 8 + dcl
                    bg, bp_ = 1 + (dc % 2), 3 + (dc % 2)
                    for k in range(2):
                        mm(psf[:, bg, :], pv[:, dcl, k, :], g1[:, k, :], k == 0, k == 1,
                           reads=[("ring", s), ("ra", k)], writes=[PSB(bg)])
                    for k in range(2):
                        mm(psf[:, bp_, :], pv[:, dcl, 2 + k, :], pb[:, k, :], k == 0, k == 1,
                           reads=[("ring", s), ("ra", 2 + k)], writes=[PSB(bp_)])
                    i = dc % 2
                    act(t_f[i], psf[:, bg, :], AF.Sigmoid, reads=[PSB(bg)], writes=[("tf", i)])
                    tt("dve", t_f[i], t_f[i], psf[:, bp_, :], ALU.mult, reads=[("tf", i), PSB(bp_)], writes=[("tf", i)])
                    tt("dve", xt[:, dc, :], xt[:, dc, :], t_f[i], ALU.add, reads=[("xt", dc), ("tf", i)], writes=[("xt", dc)])
            if S.dry:
                return
            if l == L - 1:
                to = (row0 - cfg.HALO) * GW
                dma("act", out[:, to:to + TT].rearrange("(k p) t -> p k t", p=128), xt, "xst",
                    reads=[("xt", k) for k in range(KC)], writes=[("out", row0)])
            else:
                dma("act", xs[:, t0:t0 + TT].rearrange("(k p) t -> p k t", p=128), xt, "xst",
                    reads=[("xt", k) for k in range(KC)], writes=blocks("xs", row0, row0 + 8))

        def layers():
            for l in range(L):
                k0, k1 = cfg.K(l)
                for row0 in range(k0, k1, 8):
                    phase1(l, row0)
                S.barrier()
                r0, r1 = cfg.R(l)
                for row0 in range(r0, r1, 8):
                    phaseA(l, row0)
                    S.barrier()
                    phaseB(l, row0)
                    S.barrier()

        epsc = sb("epsc", [128, 2], F32)
        S.dry = True
        layers()
        S.dry = False

        dma("act", auxs[:, :], aux_in[:, :], "aux", reads=[], writes=[("aux",)])
        S.add("dve", lambda e: e.tensor_copy(out=identb[:, :], in_=A("ident")), reads=[("aux",)], writes=[("c0",)])
        S.add("dve", lambda e: e.tensor_copy(out=onesb[:, :], in_=A("ones")), reads=[("aux",)], writes=[("c1",)])
        S.add("dve", lambda e: e.tensor_copy(out=blkb[:, :], in_=A("blk")), reads=[("aux",)], writes=[("c2",)])
        S.add("dve", lambda e: e.memset(epsc[:, 0:1], EPS), reads=[], writes=[("c3",)])
        S.add("dve", lambda e: e.memset(epsc[:, 1:2], 64.0 * EPS), reads=[], writes=[("c4",)])
        for l in range(L):
            for nm in ["win", "bt", "wpl", "wg", "wbr", "wo", "wup", "wdn", "wpd", "wpu"]:
                src = w32[(nm, l)]; dst = w16[(nm, l)]
                n = src.shape[0]
                per = src.shape[1] * src.shape[2]
                step = max(1, (8 * 1024 * 1024) // per)
                for a in range(0, n, step):
                    b = min(n, a + step)
                    S.add("pool", lambda e, a=a, b=b, src=src, dst=dst: e.dma_start(out=dst[a:b], in_=src[a:b], max_dma_last_dim=4096),
                          reads=[], writes=[("wbf", (nm, l)), ("castq",)], dma="cast", ring=True)
        S.barrier()
        layers()
        S.finalize()

        sems = {}
        def sem(key):
            if key not in sems:
                sems[key] = es.enter_context(nc.semaphore(f"s_{key}"))
            return sems[key]
        for e in COMPUTE:
            sem(e)
        final_tokens = {}
        for e in QUEUES:
            for op in S.q[e]:
                if op.is_dma:
                    final_tokens[op.dsem] = max(final_tokens.get(op.dsem, 0), op.dval)

        def replay(qname, eng, tail=False):
            waited = {}
            for op in S.q[qname]:
                need = {}
                for d in op.deps:
                    if d.is_dma:
                        k, v = d.dsem, d.dval
                    else:
                        k, v = d.eng, d.seq
                    if v > need.get(k, 0):
                        need[k] = v
                for k, v in need.items():
                    if waited.get(k, 0) < v:
                        eng.wait_ge(sem(k), v)
                        waited[k] = v
                ins = op.fn(eng)
                if op.is_dma:
                    ins.then_inc(sem(op.dsem), 16)
                elif op.needed:
                    ins.then_inc(sem(op.eng), 1)
            if tail:
                for k, v in final_tokens.items():
                    if waited.get(k, 0) < v:
                        eng.wait_ge(sem(k), v)

        block = es.enter_context(nc.Block())

        @block.tensor
        def _(e):
            replay("pe", e)

        @block.vector
        def _(e):
            replay("dve", e)

        @block.gpsimd
        def _(e):
            replay("pool", e)

        @block.sync
        def _(e):
            replay("sp", e)

        @block.scalar
        def _(e):
            replay("act", e, tail=True)

    return nc


def _run(cfg, inputs, nc=None):
    shared, cores = _host_prepare(cfg, inputs)
    if nc is None:
        nc = build_program(cfg)
    in_maps = []
    for c in range(cfg.NC):
        m = dict(shared)
        m.update(cores[c])
        in_maps.append(m)
    res = run_bass_kernel_spmd(nc, in_maps, core_ids=list(range(cfg.NC)))
    outs = [np.asarray(res.results[c]["out"]) for c in range(cfg.NC)]
    full = np.concatenate([o.T for o in outs], axis=0)
    return np.ascontiguousarray(full[None].astype(np.float32))


_PER_LAYER = ("p", "norm_mix", "w_in", "w_pool", "pool_scale", "q_norm", "k_norm", "rpb", "w_branch_pool",
              "w_branch_attn", "w_gate", "w_out", "norm_mlp", "w_up", "w_down", "norm_ple",
              "w_ple_gate_down", "w_ple_gate_up", "w_ple_proj")


def kernel(**inputs):
    depth = int(np.asarray(inputs["w_in"]).shape[0])
    cfg = Cfg(D=4096, L=1, NC=4, NR=256)
    nc = build_program(cfg)
    x = np.asarray(inputs["x"], dtype=np.float32)
    for l in range(depth):
        inp = {"x": x}
        for k in _PER_LAYER:
            inp[k] = np.asarray(inputs[k])[l:l + 1]
        x = _run(cfg, inp, nc=nc)
    return x
```

```python
import numpy as np
import concourse.bass as bass
import concourse.mybir as mybir
from concourse.bass_utils import run_bass_kernel_spmd

F32 = mybir.dt.float32
BF16 = mybir.dt.bfloat16
AF = mybir.ActivationFunctionType
ALU = mybir.AluOpType

NEG = -30000.0
EPS = 1e-6
GW = 64
TT = 512


class Cfg:
    def __init__(self, D=4096, L=4, NC=4, NR=256, DFF=None, PLE=256, RANK=256):
        self.D = D; self.L = L; self.NC = NC; self.NR = NR
        self.RO = NR // NC
        assert self.RO % 8 == 0 and NR >= 8
        self.PW = D // 4; self.AW = D // 4
        self.NH = self.AW // 64
        self.KC = D // 128
        self.PC = self.PW // 128
        self.AC = self.AW // 128
        self.GC = self.PC // 4
        assert self.GC >= 1
        self.DFF = DFF or 4 * D
        self.FC = self.DFF // 128
        self.SLAB = 16
        self.NSLAB = self.FC // self.SLAB
        self.PLE = PLE; self.RANK = RANK
        assert PLE == 256 and RANK == 256
        self.HALO = 4 * L
        self.ROWS0 = self.RO + 2 * self.HALO
        self.T0 = self.ROWS0 * GW
        self.NSLOT = 6
        self.WSZ = 32 * 128
        assert self.KC <= 32

    def K(self, l):
        return (4 * l, self.ROWS0 - 4 * l)

    def R(self, l):
        return (4 * (l + 1), self.ROWS0 - 4 * (l + 1))

    def pairs(self, rho):
        lo, hi = rho - 4, rho + 3
        H, RO = self.HALO, self.RO
        if H <= rho < H + 4:
            hi = max(hi, H + 7)
        if H + RO - 3 <= rho < H + RO:
            lo = min(lo, H + RO - 8)
        lo -= lo % 2
        return list(range(lo, hi + 1, 2))

    def special(self, rho):
        H, RO = self.HALO, self.RO
        return (H <= rho < H + 4) or (H + RO - 3 <= rho < H + RO)


def _tile_w(w, kk_per_tile=None):
    K, N = w.shape
    return np.ascontiguousarray(w.reshape(K // 128, 128, N // 128, 128).transpose(2, 1, 0, 3))


def _aux_layout(cfg):
    off = {}
    n = 0
    def add(name, width):
        nonlocal n
        off[name] = (n, width); n += width
    add("ident", 128); add("ones", 128); add("blk", 128)
    add("g", cfg.L * 3 * cfg.KC)
    add("qk", cfg.L * 2)
    add("ps", cfg.L * cfg.PC)
    add("rm", cfg.ROWS0 * 6)
    add("pc", 4 * 16)
    return off, n


def _build_bias_tables(cfg, rpb_l):
    NH = cfg.NH
    ck = np.arange(64)[:, None]; cq = np.arange(64)[None, :]
    c0 = np.clip(cq - 8, 0, 64 - 16)
    colvalid = (ck >= c0) & (ck < c0 + 16)
    dc = np.clip(ck - cq + 15, 0, 30)
    out = np.full((16, 128, NH, 64), NEG, np.float32)
    def fill(t, dr0, mask_first=False, mask_second=False):
        for half in range(2):
            if (half == 0 and mask_first) or (half == 1 and mask_second):
                continue
            dr = dr0 + half
            vals = rpb_l[:, dr + 7][:, dc]
            vals = np.where(colvalid[None], vals, np.float32(NEG))
            out[t, half * 64:(half + 1) * 64] = vals.transpose(1, 0, 2)
    for dr0 in range(-7, 7):
        fill(dr0 + 7, dr0)
    fill(14, -5, mask_first=True)
    fill(15, 3, mask_second=True)
    return out


def _host_prepare(cfg, inp):
    L, D, KC = cfg.L, cfg.D, cfg.KC
    f = lambda a: np.asarray(a, dtype=np.float32)
    shared = {}
    w_in = f(inp["w_in"]); w_gate = f(inp["w_gate"]); w_out = f(inp["w_out"])
    w_up = f(inp["w_up"]); w_down = f(inp["w_down"])
    w_bp = f(inp["w_branch_pool"]); w_ba = f(inp["w_branch_attn"])
    w_pd = f(inp["w_ple_gate_down"]); w_pu = f(inp["w_ple_gate_up"]); w_pp = f(inp["w_ple_proj"])
    w_pool = f(inp["w_pool"]); rpb = f(inp["rpb"])
    WS = cfg.WSZ
    def pad_tiles(t):
        n, p, x = t.shape
        if x == WS:
            return np.ascontiguousarray(t)
        o = np.zeros((n, p, WS), np.float32); o[:, :, :x] = t
        return o
    for l in range(L):
        tiles = {}
        tiles["win"] = _tile_w(w_in[l]).reshape(-1, 128, KC * 128)
        tiles["wg"] = _tile_w(w_gate[l]).reshape(-1, 128, KC * 128)
        tiles["wo"] = _tile_w(w_out[l]).reshape(-1, 128, KC * 128)
        tiles["wup"] = _tile_w(w_up[l]).reshape(-1, 128, KC * 128)
        wd = w_down[l].reshape(cfg.NSLAB, cfg.SLAB, 128, KC // 2, 2, 128)
        tiles["wdn"] = np.ascontiguousarray(wd.transpose(0, 3, 2, 4, 1, 5)).reshape(-1, 128, 2 * cfg.SLAB * 128)
        bp = w_bp[l].reshape(cfg.PC, 128, KC, 128)
        ba = w_ba[l].reshape(cfg.AC, 128, KC, 128)
        br = np.concatenate([bp.transpose(2, 1, 0, 3), ba.transpose(2, 1, 0, 3)], axis=2)
        tiles["wbr"] = np.ascontiguousarray(br).reshape(KC, 128, (cfg.PC + cfg.AC) * 128)
        tiles["wpd"] = _tile_w(w_pd[l]).reshape(-1, 128, KC * 128)
        pu = w_pu[l].reshape(2, 128, KC // 8, 8, 128)
        pp = w_pp[l].reshape(2, 128, KC // 8, 8, 128)
        pup = np.concatenate([pu.transpose(2, 1, 3, 0, 4), pp.transpose(2, 1, 3, 0, 4)], axis=3)
        tiles["wpu"] = np.ascontiguousarray(pup).reshape(KC // 8, 128, 8 * 4 * 128)
        GC = cfg.GC
        wp = w_pool[l].reshape(4, GC, 128, GC, 128)
        tiles["wpl"] = np.ascontiguousarray(wp.transpose(2, 0, 1, 3, 4)).reshape(1, 128, 4 * GC * GC * 128)
        for k, v in tiles.items():
            shared[f"{k}{l}"] = np.ascontiguousarray(v)
        shared[f"bt{l}"] = _build_bias_tables(cfg, rpb[l]).reshape(16, 128, cfg.NH * 64)
    off, naux = _aux_layout(cfg)
    x = f(inp["x"])[0]
    p = f(inp["p"])[:, 0]
    S = cfg.NR * GW
    cores = []
    for c in range(cfg.NC):
        base = c * cfg.RO - cfg.HALO
        aux = np.zeros((128, naux), np.float32)
        o, w = off["ident"]; aux[:, o:o + w] = np.eye(128, dtype=np.float32)
        o, w = off["ones"]; aux[:, o:o + w] = 1.0
        o, w = off["blk"]
        aux[0:64, o:o + 64] = 1.0; aux[64:128, o + 64:o + 128] = 1.0
        o, w = off["g"]
        for l in range(L):
            for j, nm in enumerate(("norm_mix", "norm_mlp", "norm_ple")):
                g = f(inp[nm])[l].reshape(KC, 128).T
                aux[:, o + (l * 3 + j) * KC: o + (l * 3 + j + 1) * KC] = g
        o, w = off["qk"]
        for l in range(L):
            aux[:, o + 2 * l] = np.tile(f(inp["q_norm"])[l], 2)
            aux[:, o + 2 * l + 1] = np.tile(f(inp["k_norm"])[l], 2)
        o, w = off["ps"]
        for l in range(L):
            aux[:, o + l * cfg.PC: o + (l + 1) * cfg.PC] = f(inp["pool_scale"])[l].reshape(cfg.PC, 128).T
        o, w = off["rm"]
        for rho in range(cfg.ROWS0):
            gq = rho + base
            if not (0 <= gq < cfg.NR):
                continue
            r0 = min(max(gq - 4, 0), cfg.NR - 8)
            for j, pr in enumerate(cfg.pairs(rho)):
                for half in range(2):
                    gk = pr + half + base
                    if not (r0 <= gk < r0 + 8):
                        aux[half * 64:(half + 1) * 64, o + rho * 6 + j] = NEG
        o, w = off["pc"]
        aux[:, o:o + w] = 1.0
        for g, wd_ in enumerate((2, 4, 8, 16)):
            for i in range(8):
                if c == 0:
                    t = i
                    lo = max(t - wd_ // 2, 0); hi = min(t + wd_ // 2, S)
                    aux[:, o + g * 16 + i] = np.float32(wd_) / np.float32(hi - lo)
                if c == cfg.NC - 1:
                    t = S - 8 + i
                    lo = max(t - wd_ // 2, 0); hi = min(t + wd_ // 2, S)
                    aux[:, o + g * 16 + 8 + i] = np.float32(wd_) / np.float32(hi - lo)
        rows = np.arange(cfg.ROWS0) + base
        ok = (rows >= 0) & (rows < cfg.NR)
        xT = np.zeros((D, cfg.T0), np.float32)
        pT = np.zeros((L, 256, cfg.T0), np.float32)
        r_lo = max(base, 0); r_hi = min(base + cfg.ROWS0, cfg.NR)
        t_lo = (r_lo - base) * GW; t_hi = (r_hi - base) * GW
        xT[:, t_lo:t_hi] = x[r_lo * GW:r_hi * GW].T
        for l in range(L):
            pT[l, :, t_lo:t_hi] = p[l, r_lo * GW:r_hi * GW].T
        valid = np.zeros((128, cfg.T0), np.float32)
        valid[:, t_lo:t_hi] = 1.0
        cores.append({"xT": xT, "pT": pT.reshape(L * 256, cfg.T0), "aux": aux, "valid": valid})
    return shared, cores


COMPUTE = ("pe", "act", "dve", "pool")
QUEUES = ("pe", "act", "dve", "pool", "sp")


class Op:
    __slots__ = ("eng", "fn", "deps", "needed", "seq", "is_dma", "dsem", "dval", "idx")


class Sched:
    def __init__(self):
        self.q = {e: [] for e in QUEUES}
        self.last_w = {}
        self.readers = {}
        self.dry = False
        self.bar = {e: set() for e in COMPUTE}
        self.recent_dma = []
        self.dcount = {}
        self.last_on = {}
        self.n = 0

    def add(self, eng, fn, reads=(), writes=(), dma=None, ring=False):
        if self.dry:
            return None
        op = Op()
        op.eng = eng; op.fn = fn; op.needed = False; op.seq = 0; op.idx = self.n; self.n += 1
        op.is_dma = dma is not None; op.dsem = dma; op.dval = 0
        deps = set()
        for r in reads:
            lw = self.last_w.get(r)
            if lw is not None:
                deps.add(lw)
        for w in writes:
            lw = self.last_w.get(w)
            if lw is not None:
                deps.add(lw)
            rd = self.readers.get(w)
            if rd:
                deps.update(rd[0].values()); deps.update(rd[1])
        if eng in self.bar and self.bar[eng]:
            deps |= self.bar[eng]; self.bar[eng] = set()
        op.deps = [d for d in deps if not (d.eng == "pe" and eng == "pe" and not d.is_dma and not op.is_dma)]
        if op.is_dma:
            self.dcount[dma] = self.dcount.get(dma, 0) + 16
            op.dval = self.dcount[dma]
            if not ring:
                self.recent_dma.append(op)
        for r in reads:
            rd = self.readers.setdefault(r, ({}, []))
            if op.is_dma:
                rd[1].append(op)
            else:
                rd[0][eng] = op
        for w in writes:
            self.last_w[w] = op
            self.readers[w] = ({}, [])
        self.q[eng].append(op)
        if not op.is_dma:
            self.last_on[eng] = op
        return op

    def barrier(self):
        if self.dry:
            return
        for e in COMPUTE:
            s = set(self.recent_dma)
            for e2 in COMPUTE:
                if e2 != e and e2 in self.last_on:
                    s.add(self.last_on[e2])
            self.bar[e] |= s
        self.recent_dma = []

    def finalize(self):
        for e in QUEUES:
            for op in self.q[e]:
                for d in op.deps:
                    d.needed = True
        for e in QUEUES:
            c = 0
            for op in self.q[e]:
                if op.needed and not op.is_dma:
                    c += 1; op.seq = c


def build_program(cfg):
    nc = bass.Bass("TRN2", target_bir_lowering=False)
    L, D, KC, PC, AC, NH, T0 = cfg.L, cfg.D, cfg.KC, cfg.PC, cfg.AC, cfg.NH, cfg.T0
    WS = cfg.WSZ
    off, naux = _aux_layout(cfg)

    xT_in = nc.dram_tensor("xT", [D, T0], F32, kind="ExternalInput").ap()
    pT_in = nc.dram_tensor("pT", [L * 256, T0], F32, kind="ExternalInput").ap()
    aux_in = nc.dram_tensor("aux", [128, naux], F32, kind="ExternalInput").ap()
    valid_in = nc.dram_tensor("valid", [128, T0], F32, kind="ExternalInput").ap()
    out = nc.dram_tensor("out", [D, cfg.RO * GW], F32, kind="ExternalOutput").ap()
    wnames = ["win", "wg", "wo", "wup", "wdn", "wbr", "wpd", "wpu", "wpl"]
    wcount = {"win": KC, "wg": 2 * KC, "wo": KC, "wup": cfg.FC, "wdn": cfg.NSLAB * (KC // 2),
              "wbr": KC, "wpd": 2, "wpu": KC // 8, "wpl": 1}
    wwidth = {"win": KC * 128, "wg": KC * 128, "wo": KC * 128, "wup": KC * 128, "wdn": 2 * cfg.SLAB * 128,
              "wbr": (PC + AC) * 128, "wpd": KC * 128, "wpu": 8 * 4 * 128, "wpl": 4 * cfg.GC * cfg.GC * 128}
    w32 = {}; w16 = {}
    for l in range(L):
        for nm in wnames:
            assert wwidth[nm] <= WS
            w32[(nm, l)] = nc.dram_tensor(f"{nm}{l}", [wcount[nm], 128, wwidth[nm]], F32, kind="ExternalInput").ap()
            w16[(nm, l)] = nc.dram_tensor(f"{nm}{l}_bf", [wcount[nm], 128, wwidth[nm]], BF16).ap()
        w32[("bt", l)] = nc.dram_tensor(f"bt{l}", [16, 128, NH * 64], F32, kind="ExternalInput").ap()
        w16[("bt", l)] = nc.dram_tensor(f"bt{l}_bf", [16, 128, NH * 64], BF16).ap()
    xs = nc.dram_tensor("xs", [D, T0], F32).ap()
    qs = nc.dram_tensor("qs", [cfg.AW, T0], BF16).ap()
    ks = nc.dram_tensor("ks", [cfg.AW, T0], BF16).ap()
    us = nc.dram_tensor("us", [cfg.PW, T0], F32).ap()
    vs = nc.dram_tensor("vs", [T0, cfg.AW], BF16).ap()
    ms = nc.dram_tensor("ms", [D, TT], BF16).ap()

    S = Sched()

    import contextlib
    with contextlib.ExitStack() as es:
        def sb(name, shape, dt):
            return es.enter_context(nc.sbuf_tensor(name, shape, dt))
        ring = sb("ring", [128, cfg.NSLOT, WS], BF16)
        auxs = sb("auxs", [128, naux], F32)
        identb = sb("identb", [128, 128], BF16)
        onesb = sb("onesb", [128, 128], BF16)
        blkb = sb("blkb", [128, 128], BF16)
        Rx = sb("Rx", [128, KC * TT], F32)
        Rh = sb("Rh", [128, KC * TT], BF16)
        Ra = sb("Ra", [128, max(cfg.SLAB, PC + AC) * TT], BF16)
        Rt = sb("Rt", [128, 6784], F32)
        psf = es.enter_context(nc.psum_tensor("psf", [128, 7, 512], F32))
        psb = es.enter_context(nc.psum_tensor("psb", [128, 1024], BF16))

        xt = Rx[:, :].rearrange("p (k t) -> p k t", t=TT)
        hb = Rh[:, :].rearrange("p (k t) -> p k t", t=TT)
        RxB = Rx[:, :].bitcast(BF16)
        o_ = 0
        kw = RxB[:, o_:o_ + AC * 1024].rearrange("p (c t) -> p c t", t=1024); o_ += AC * 1024
        vw = RxB[:, o_:o_ + 8 * cfg.AW].rearrange("p (j f) -> p j f", f=cfg.AW); o_ += 8 * cfg.AW
        qw = RxB[:, o_:o_ + AC * TT].rearrange("p (c t) -> p c t", t=TT); o_ += AC * TT
        assert o_ % 2 == 0
        uw = Rx[:, o_ // 2:o_ // 2 + PC * 528].rearrange("p (c t) -> p c t", t=528)
        assert o_ // 2 + PC * 528 <= KC * TT
        btw = Rh[:, 0:16 * NH * 64].rearrange("p (t h q) -> p t h q", t=16, h=NH)
        assert 16 * NH * 64 <= KC * TT
        aT = Ra[:, 0:cfg.SLAB * TT].rearrange("p (k t) -> p k t", t=TT)
        poolo = Ra[:, 0:PC * TT].rearrange("p (k t) -> p k t", t=TT)
        attno = Ra[:, PC * TT:(PC + AC) * TT].rearrange("p (k t) -> p k t", t=TT)
        RtB = Rt[:, :].bitcast(BF16)
        t_f = [Rt[:, i * 512:(i + 1) * 512] for i in range(4)]
        sqb = [RtB[:, 4096 + i * 512:4096 + (i + 1) * 512] for i in range(2)]
        stg = [RtB[:, 5120 + i * 512:5120 + (i + 1) * 512] for i in range(2)]
        rstd = Rt[:, 3072:3584]
        vtmp = Rt[:, 3584:4096]

        def A(name):
            o, w = off[name]
            return auxs[:, o:o + w]

        def gcol(l, j, kc):
            o, _ = off["g"]
            c = o + (l * 3 + j) * KC + kc
            return auxs[:, c:c + 1]

        class WQ:
            def __init__(self):
                self.reqs = []; self.i = 0; self.issued = 0

            def get(self):
                if S.dry:
                    self.reqs.append(self._cur)
                    return None
                i = self.i; self.i += 1
                assert self.reqs[i] == self._cur, (i, self.reqs[i], self._cur)
                while self.issued < min(len(self.reqs), i + cfg.NSLOT):
                    self._issue(self.issued); self.issued += 1
                return i % cfg.NSLOT

            def req(self, key, idx, n=None):
                self._cur = (key, idx, n if n is not None else wwidth[key[0]])
                return self.get()

            def _issue(self, j):
                key, idx, n = self.reqs[j]
                slot = j % cfg.NSLOT
                src = w16[key][idx, :, 0:n]
                dst = ring[:, slot, 0:n]
                S.add("sp", lambda e, d=dst, s=src, k=f"ring{slot}": e.dma_start(out=d, in_=s),
                      reads=[("wbf", key)], writes=[("ring", slot)], dma=f"ring{slot}", ring=True)

        wq = WQ()

        def wtile(slot, n=WS):
            return ring[:, slot, 0:n]

        def mm(out_ap, lhsT, rhs, start, stop, reads, writes):
            S.add("pe", lambda e: e.matmul(out_ap, lhsT=lhsT, rhs=rhs, start=start, stop=stop),
                  reads=reads, writes=writes)

        def act(out_ap, in_ap, func, reads, writes, bias=None, scale=None):
            kw_ = {}
            if bias is not None: kw_["bias"] = bias
            if scale is not None: kw_["scale"] = scale
            S.add("act", lambda e: e.activation(out=out_ap, in_=in_ap, func=func, **kw_), reads=reads, writes=writes)

        def tt(eng, out_ap, a, b, op, reads, writes):
            S.add(eng, lambda e: e.tensor_tensor(out=out_ap, in0=a, in1=b, op=op), reads=reads, writes=writes)

        def stt(out_ap, in0, scalar, in1, op0, op1, reads, writes):
            S.add("dve", lambda e: e.scalar_tensor_tensor(out=out_ap, in0=in0, scalar=scalar, in1=in1, op0=op0, op1=op1),
                  reads=reads, writes=writes)

        def dma(q, out_ap, in_ap, key, reads, writes):
            S.add(q, lambda e: e.dma_start(out=out_ap, in_=in_ap), reads=reads, writes=writes, dma=key)

        def blocks(prefix, r0, r1):
            return [(prefix, b) for b in range(r0 // 4, (r1 + 3) // 4)]

        PSB = lambda b: ("ps", b)

        def emit_norm(l, j, t0, use_valid):
            for kc in range(KC):
                b = sqb[kc % 2]
                act(b, xt[:, kc, :], AF.Square, reads=[("xt", kc)], writes=[("sqb", kc % 2)])
                mm(psf[:, 0, :], onesb[:, :], b, kc == 0, kc == KC - 1, reads=[("sqb", kc % 2)], writes=[PSB(0)])
            act(t_f[0], psf[:, 0, :], AF.Sqrt, reads=[PSB(0)], writes=[("tf", 0)], bias=epsc[:, 0:1], scale=1.0 / D)
            S.add("dve", lambda e: e.reciprocal(out=rstd, in_=t_f[0]), reads=[("tf", 0)], writes=[("rstd",)])
            if use_valid:
                dma("act", vtmp, valid_in[:, t0:t0 + TT], "vld", reads=[], writes=[("vtmp",)])
                tt("dve", rstd, rstd, vtmp, ALU.mult, reads=[("rstd",), ("vtmp",)], writes=[("rstd",)])
            for kc in range(KC):
                stt(hb[:, kc, :], xt[:, kc, :], gcol(l, j, kc), rstd, ALU.mult, ALU.mult,
                    reads=[("xt", kc), ("rstd",)], writes=[("hb", kc)])

        def proj(bank, slot, nk, rhs_fn, rhs_res, wview=None):
            wv = wview if wview is not None else wtile(slot, nk * 128).rearrange("p (k c) -> p k c", c=128)
            for k in range(nk):
                mm(psf[:, bank, :], wv[:, k, :], rhs_fn(k), k == 0, k == nk - 1,
                   reads=[("ring", slot), rhs_res(k)], writes=[PSB(bank)])

        def phase1(l, row0):
            t0 = row0 * GW
            src = xT_in if l == 0 else xs
            dma("act", xt, src[:, t0:t0 + TT].rearrange("(k p) t -> p k t", p=128), "xt",
                reads=blocks("xs", row0, row0 + 8), writes=[("xt", k) for k in range(KC)])
            emit_norm(l, 0, t0, True)
            qo, _ = off["qk"]
            noc = PC + 3 * AC
            for oc in range(noc):
                slot = wq.req(("win", l), oc)
                bank = 1 + (oc % 2)
                if not S.dry:
                    proj(bank, slot, KC, lambda k: hb[:, k, :], lambda k: ("hb", k))
                    P = psf[:, bank, :]
                    if oc < PC:
                        i = oc % 2
                        act(t_f[1 + i], P, AF.Copy, reads=[PSB(bank)], writes=[("tf", 1 + i)])
                        dma("act", us[oc * 128:(oc + 1) * 128, t0:t0 + TT], t_f[1 + i], f"ust{i}",
                            reads=[("tf", 1 + i)], writes=blocks("us", row0, row0 + 8))
                    elif oc < PC + 2 * AC:
                        isq = oc < PC + AC
                        c = (oc - PC) % AC
                        i = oc % 2
                        act(sqb[i], P, AF.Square, reads=[PSB(bank)], writes=[("sqb", i)])
                        mm(psf[:, 5, :], blkb[:, :], sqb[i], True, True, reads=[("sqb", i)], writes=[PSB(5)])
                        if isq:
                            act(t_f[3], psf[:, 5, :], AF.Sqrt, reads=[PSB(5)], writes=[("tf", 3)], bias=epsc[:, 1:2], scale=1.0)
                        else:
                            act(t_f[3], psf[:, 5, :], AF.Sqrt, reads=[PSB(5)], writes=[("tf", 3)], bias=epsc[:, 0:1], scale=1.0 / 64)
                        S.add("dve", lambda e: e.reciprocal(out=t_f[3], in_=t_f[3]), reads=[("tf", 3)], writes=[("tf", 3)])
                        gq = auxs[:, qo + 2 * l + (0 if isq else 1): qo + 2 * l + (0 if isq else 1) + 1]
                        stt(stg[i], P, gq, t_f[3], ALU.mult, ALU.mult, reads=[PSB(bank), ("tf", 3)], writes=[("stg", i)])
                        dst = (qs if isq else ks)[c * 128:(c + 1) * 128, t0:t0 + TT]
                        dma("act", dst, stg[i], f"qkst{i}", reads=[("stg", i)],
                            writes=blocks("qs" if isq else "ks", row0, row0 + 8))
                    else:
                        c = oc - PC - 2 * AC
                        i = oc % 2
                        act(stg[i], P, AF.Copy, reads=[PSB(bank)], writes=[("stg", i)])
                        for sj in range(4):
                            S.add("pe", lambda e, sj=sj, i=i: e.transpose(psb[:, sj * 128:(sj + 1) * 128], stg[i][:, sj * 128:(sj + 1) * 128], identb[:, :]),
                                  reads=[("stg", i)], writes=[("psb",)])
                        vst = Ra[:, 0:4 * cfg.AW].rearrange("p (s f) -> p s f", f=cfg.AW)
                        S.add("dve", lambda e, c=c: e.tensor_copy(out=vst[:, :, c * 128:(c + 1) * 128],
                                                                 in_=psb[:, 0:512].rearrange("p (s f) -> p s f", f=128)),
                              reads=[("psb",)], writes=[("vst",)])
                        if c == AC - 1:
                            dma("act", vs[t0:t0 + TT, :].rearrange("(s p) f -> p s f", p=128), vst, "vst",
                                reads=[("vst",)], writes=blocks("vs", row0, row0 + 8))

        def phaseA(l, row0):
            t0 = row0 * GW
            kr0 = row0 - 4
            dma("act", kw, ks[:, kr0 * GW:kr0 * GW + 1024].rearrange("(c p) t -> p c t", p=128), "kw",
                reads=blocks("ks", kr0, kr0 + 16), writes=[("kw",)])
            dma("act", vw, vs[kr0 * GW:kr0 * GW + 1024, :].rearrange("(j p) f -> p j f", p=128), "vw",
                reads=blocks("vs", kr0, kr0 + 16), writes=[("vw",)])
            dma("act", qw, qs[:, t0:t0 + TT].rearrange("(c p) t -> p c t", p=128), "qw",
                reads=blocks("qs", row0, row0 + 8), writes=[("qw",)])
            dma("act", uw, us[:, t0 - 8:t0 + TT + 8].rearrange("(c p) t -> p c t", p=128), "uw",
                reads=blocks("us", row0 - 1, row0 + 9), writes=[("uw",)])
            dma("act", btw, w16[("bt", l)].rearrange("t p (h q) -> p t h q", q=64), "btw",
                reads=[("wbf", ("bt", l))], writes=[("btw",)])
            rmo, _ = off["rm"]
            pT = [RtB[:, 8192 + i * 512: 8192 + i * 512 + 384] for i in range(2)]
            sbank = [1, 2]
            n_sc = 0
            for hc in range(AC):
                ob, db = 3 + (hc % 2) * 2, 4 + (hc % 2) * 2
                for rr in range(8):
                    rho = row0 + rr
                    prs = cfg.pairs(rho)
                    spec = cfg.special(rho)
                    for hh in range(2):
                        h = hc * 2 + hh
                        hp = slice(hh * 64, hh * 64 + 64)
                        pi_ = n_sc % 2; sbk = sbank[pi_]; pt = pT[pi_]; n_sc += 1
                        for j, pr in enumerate(prs):
                            kt = (pr - kr0) * GW
                            dr0 = pr - rho
                            if spec:
                                tab = dr0 + 7
                            else:
                                tab = 14 if (j == 0 and dr0 == -5) else (15 if (dr0 == 3) else dr0 + 7)
                            sc = psf[:, sbk, j * 64:(j + 1) * 64]
                            mm(sc, kw[hp, hc, kt:kt + 128], qw[hp, hc, rr * 64:(rr + 1) * 64], True, False,
                               reads=[("kw",), ("qw",)], writes=[PSB(sbk)])
                            mm(sc, identb[:, :], btw[:, tab, h, :], False, True,
                               reads=[("btw",)], writes=[PSB(sbk)])
                        if spec:
                            for j in range(len(prs)):
                                col = rmo + rho * 6 + j
                                act(pt[:, j * 64:(j + 1) * 64], psf[:, sbk, j * 64:(j + 1) * 64], AF.Exp,
                                    reads=[PSB(sbk)], writes=[("pt", pi_)], bias=auxs[:, col:col + 1], scale=1.0)
                        else:
                            n = len(prs) * 64
                            act(pt[:, 0:n], psf[:, sbk, 0:n], AF.Exp, reads=[PSB(sbk)], writes=[("pt", pi_)])
                        for j, pr in enumerate(prs):
                            vj = (pr - kr0) // 2
                            mm(psf[hp, ob, rr * 64:(rr + 1) * 64], vw[:, vj, h * 64:(h + 1) * 64], pt[:, j * 64:(j + 1) * 64],
                               j == 0, j == len(prs) - 1, reads=[("vw",), ("pt", pi_)], writes=[PSB(ob)])
                        for j, pr in enumerate(prs):
                            mm(psf[hp, db, rr * 64:(rr + 1) * 64], onesb[:, 0:64], pt[:, j * 64:(j + 1) * 64],
                               j == 0, j == len(prs) - 1, reads=[("pt", pi_)], writes=[PSB(db)])
                i = hc % 2
                S.add("dve", lambda e, db=db, i=i: e.reciprocal(out=t_f[i], in_=psf[:, db, :]), reads=[PSB(db)], writes=[("tf", i)])
                tt("dve", attno[:, hc, :], psf[:, ob, :], t_f[i], ALU.mult, reads=[PSB(ob), ("tf", i)], writes=[("ra", PC + hc)])
            slot = wq.req(("wpl", l), 0, 4 * cfg.GC * cfg.GC * 128)
            if S.dry:
                return
            GCn = cfg.GC
            wpv = wtile(slot, 4 * GCn * GCn * 128).rearrange("p (g k o c) -> p g k o c", g=4, k=GCn, o=GCn)
            pco, _ = off["pc"]
            pso, _ = off["ps"]
            sA = Rt[:, 4608:4608 + GCn * 528].rearrange("p (c t) -> p c t", t=528)
            sB = Rt[:, 5664:5664 + GCn * 528].rearrange("p (c t) -> p c t", t=528)
            dT = RtB[:, 2048:2048 + GCn * TT].rearrange("p (c t) -> p c t", t=TT)
            H0 = cfg.HALO
            for g, wd_ in enumerate((2, 4, 8, 16)):
                U = uw[:, g * GCn:(g + 1) * GCn, :]
                kst = wd_.bit_length() - 1
                cur, cur_res = U, ("uw",)
                n = 528
                bufs = [sA, sB]
                for st_ in range(kst):
                    sh = 1 << st_
                    n2 = n - sh
                    dst = bufs[st_ % 2]
                    tt("dve", dst[:, :, 0:n2], cur[:, :, 0:n2], cur[:, :, sh:sh + n2], ALU.add,
                       reads=[cur_res], writes=[("pl", st_ % 2)])
                    cur, cur_res, n = dst, ("pl", st_ % 2), n2
                o0 = 8 - wd_ // 2
                for (rb, side) in ((H0, 0), (H0 + cfg.RO - 1, 1)):
                    if row0 <= rb < row0 + 8:
                        tl = (rb - row0) * GW + (0 if side == 0 else GW - 8)
                        cc = auxs[:, pco + g * 16 + side * 8: pco + g * 16 + side * 8 + 8]
                        for c_ in range(GCn):
                            tt("dve", cur[:, c_, o0 + tl:o0 + tl + 8], cur[:, c_, o0 + tl:o0 + tl + 8], cc, ALU.mult,
                               reads=[cur_res], writes=[cur_res])
                for c_ in range(GCn):
                    stt(dT[:, c_, :], cur[:, c_, o0:o0 + TT], 1.0 / wd_, U[:, c_, 8:8 + TT], ALU.mult, ALU.subtract,
                        reads=[cur_res, ("uw",)], writes=[("tf", 2)])
                for oc in range(GCn):
                    bank = 1 + (oc + g) % 2
                    for k in range(GCn):
                        mm(psf[:, bank, :], wpv[:, g, k, oc, :], dT[:, k, :], k == 0, k == GCn - 1,
                           reads=[("ring", slot), ("tf", 2)], writes=[PSB(bank)])
                    ch = g * GCn + oc
                    S.add("dve", lambda e, ch=ch, bank=bank: e.tensor_scalar(out=poolo[:, ch, :], in0=psf[:, bank, :],
                                                                             scalar1=auxs[:, pso + l * PC + ch: pso + l * PC + ch + 1],
                                                                             scalar2=None, op0=ALU.mult),
                          reads=[PSB(bank)], writes=[("ra", ch)])

        def phaseB(l, row0):
            t0 = row0 * GW
            src = xT_in if l == 0 else xs
            dma("act", xt, src[:, t0:t0 + TT].rearrange("(k p) t -> p k t", p=128), "xt",
                reads=blocks("xs", row0, row0 + 8), writes=[("xt", k) for k in range(KC)])
            emit_norm(l, 0, t0, False)
            nbr = PC + AC
            for dc in range(KC):
                bA, bB = 1 + (dc % 2), 3 + (dc % 2)
                sgA = wq.req(("wg", l), dc)
                if not S.dry:
                    proj(bA, sgA, KC, lambda k: hb[:, k, :], lambda k: ("hb", k))
                sgB = wq.req(("wg", l), KC + dc)
                if not S.dry:
                    proj(bB, sgB, KC, lambda k: hb[:, k, :], lambda k: ("hb", k))
                sbr = wq.req(("wbr", l), dc, nbr * 128)
                if S.dry:
                    continue
                bv = wtile(sbr, nbr * 128).rearrange("p (k c) -> p k c", c=128)
                for k in range(PC):
                    mm(psf[:, 5, :], bv[:, k, :], poolo[:, k, :], k == 0, k == PC - 1,
                       reads=[("ring", sbr), ("ra", k)], writes=[PSB(5)])
                for k in range(AC):
                    mm(psf[:, 6, :], bv[:, PC + k, :], attno[:, k, :], k == 0, k == AC - 1,
                       reads=[("ring", sbr), ("ra", PC + k)], writes=[PSB(6)])
                act(t_f[0], psf[:, bA, :], AF.Sigmoid, reads=[PSB(bA)], writes=[("tf", 0)])
                act(t_f[1], psf[:, bB, :], AF.Sigmoid, reads=[PSB(bB)], writes=[("tf", 1)])
                tt("dve", t_f[0], t_f[0], psf[:, 5, :], ALU.mult, reads=[("tf", 0), PSB(5)], writes=[("tf", 0)])
                tt("dve", t_f[1], t_f[1], psf[:, 6, :], ALU.mult, reads=[("tf", 1), PSB(6)], writes=[("tf", 1)])
                i = dc % 2
                tt("dve", stg[i], t_f[0], t_f[1], ALU.add, reads=[("tf", 0), ("tf", 1)], writes=[("stg", i)])
                dma("act", ms[dc * 128:(dc + 1) * 128, :], stg[i], f"mst{i}", reads=[("stg", i)], writes=[("ms", dc)])
            dma("act", hb, ms.rearrange("(k p) t -> p k t", p=128), "hbld",
                reads=[("ms", k) for k in range(KC)], writes=[("hb", k) for k in range(KC)])
            for dc in range(KC):
                s = wq.req(("wo", l), dc)
                if S.dry:
                    continue
                bank = 1 + (dc % 2)
                proj(bank, s, KC, lambda k: hb[:, k, :], lambda k: ("hb", k))
                tt("dve", xt[:, dc, :], xt[:, dc, :], psf[:, bank, :], ALU.add, reads=[("xt", dc), PSB(bank)], writes=[("xt", dc)])
            emit_norm(l, 1, t0, False)
            for sl in range(cfg.NSLAB):
                for fcl in range(cfg.SLAB):
                    s = wq.req(("wup", l), sl * cfg.SLAB + fcl)
                    if S.dry:
                        continue
                    bank = 1 + (fcl % 2)
                    proj(bank, s, KC, lambda k: hb[:, k, :], lambda k: ("hb", k))
                    i = fcl % 2
                    act(t_f[i], psf[:, bank, :], AF.Relu, reads=[PSB(bank)], writes=[("tf", i)])
                    act(aT[:, fcl, :], t_f[i], AF.Square, reads=[("tf", i)], writes=[("ra", fcl)])
                for dcp in range(KC // 2):
                    s = wq.req(("wdn", l), sl * (KC // 2) + dcp, 2 * cfg.SLAB * 128)
                    if S.dry:
                        continue
                    dv = wtile(s, 2 * cfg.SLAB * 128).rearrange("p (d k c) -> p d k c", d=2, k=cfg.SLAB)
                    for dcl in range(2):
                        dc = dcp * 2 + dcl
                        bank = 3 + (dc % 2)
                        for k in range(cfg.SLAB):
                            mm(psf[:, bank, :], dv[:, dcl, k, :], aT[:, k, :], k == 0, k == cfg.SLAB - 1,
                               reads=[("ring", s), ("ra", k)], writes=[PSB(bank)])
                        tt("dve", xt[:, dc, :], xt[:, dc, :], psf[:, bank, :], ALU.add,
                           reads=[("xt", dc), PSB(bank)], writes=[("xt", dc)])
            emit_norm(l, 2, t0, False)
            g1 = Ra[:, 0:2 * TT].rearrange("p (k t) -> p k t", t=TT)
            pb = Ra[:, 2 * TT:4 * TT].rearrange("p (k t) -> p k t", t=TT)
            pf = Rt[:, 1024:2048].rearrange("p (k t) -> p k t", t=TT)
            dma("act", pf, pT_in[l * 256:(l + 1) * 256, t0:t0 + TT].rearrange("(k p) t -> p k t", p=128), "pf",
                reads=[], writes=[("tf", 2), ("tf", 3)])
            S.add("dve", lambda e: e.tensor_copy(out=pb, in_=pf), reads=[("tf", 2), ("tf", 3)], writes=[("ra", 2), ("ra", 3)])
            for oc in range(2):
                s = wq.req(("wpd", l), oc)
                if S.dry:
                    continue
                bank = 1 + oc
                proj(bank, s, KC, lambda k: hb[:, k, :], lambda k: ("hb", k))
                act(g1[:, oc, :], psf[:, bank, :], AF.Copy, reads=[PSB(bank)], writes=[("ra", oc)])
            for grp in range(KC // 8):
                s = wq.req(("wpu", l), grp)
                if S.dry:
                    continue
                pv = wtile(s, 8 * 4 * 128).rearrange("p (d k c) -> p d k c", d=8, k=4)
                for dcl in range(8):
                    dc = grp * 8 + dcl
                    bg, bp_ = 1 + (dc % 2), 3 + (dc % 2)
                    for k in range(2):
                        mm(psf[:, bg, :], pv[:, dcl, k, :], g1[:, k, :], k == 0, k == 1,
                           reads=[("ring", s), ("ra", k)], writes=[PSB(bg)])
                    for k in range(2):
                        mm(psf[:, bp_, :], pv[:, dcl, 2 + k, :], pb[:, k, :], k == 0, k == 1,
                           reads=[("ring", s), ("ra", 2 + k)], writes=[PSB(bp_)])
                    i = dc % 2
                    act(t_f[i], psf[:, bg, :], AF.Sigmoid, reads=[PSB(bg)], writes=[("tf", i)])
                    tt("dve", t_f[i], t_f[i], psf[:, bp_, :], ALU.mult, reads=[("tf", i), PSB(bp_)], writes=[("tf", i)])
                    tt("dve", xt[:, dc, :], xt[:, dc, :], t_f[i], ALU.add, reads=[("xt", dc), ("tf", i)], writes=[("xt", dc)])
            if S.dry:
                return
            if l == L - 1:
                to = (row0 - cfg.HALO) * GW
                dma("act", out[:, to:to + TT].rearrange("(k p) t -> p k t", p=128), xt, "xst",
                    reads=[("xt", k) for k in range(KC)], writes=[("out", row0)])
            else:
                dma("act", xs[:, t0:t0 + TT].rearrange("(k p) t -> p k t", p=128), xt, "xst",
                    reads=[("xt", k) for k in range(KC)], writes=blocks("xs", row0, row0 + 8))

        def layers():
            for l in range(L):
                k0, k1 = cfg.K(l)
                for row0 in range(k0, k1, 8):
                    phase1(l, row0)
                S.barrier()
                r0, r1 = cfg.R(l)
                for row0 in range(r0, r1, 8):
                    phaseA(l, row0)
                    S.barrier()
                    phaseB(l, row0)
                    S.barrier()

        epsc = sb("epsc", [128, 2], F32)
        S.dry = True
        layers()
        S.dry = False

        dma("act", auxs[:, :], aux_in[:, :], "aux", reads=[], writes=[("aux",)])
        S.add("dve", lambda e: e.tensor_copy(out=identb[:, :], in_=A("ident")), reads=[("aux",)], writes=[("c0",)])
        S.add("dve", lambda e: e.tensor_copy(out=onesb[:, :], in_=A("ones")), reads=[("aux",)], writes=[("c1",)])
        S.add("dve", lambda e: e.tensor_copy(out=blkb[:, :], in_=A("blk")), reads=[("aux",)], writes=[("c2",)])
        S.add("dve", lambda e: e.memset(epsc[:, 0:1], EPS), reads=[], writes=[("c3",)])
        S.add("dve", lambda e: e.memset(epsc[:, 1:2], 64.0 * EPS), reads=[], writes=[("c4",)])
        for l in range(L):
            for nm in ["win", "bt", "wpl", "wg", "wbr", "wo", "wup", "wdn", "wpd", "wpu"]:
                src = w32[(nm, l)]; dst = w16[(nm, l)]
                n = src.shape[0]
                per = src.shape[1] * src.shape[2]
                step = max(1, (8 * 1024 * 1024) // per)
                for a in range(0, n, step):
                    b = min(n, a + step)
                    S.add("pool", lambda e, a=a, b=b, src=src, dst=dst: e.dma_start(out=dst[a:b], in_=src[a:b], max_dma_last_dim=4096),
                          reads=[], writes=[("wbf", (nm, l)), ("castq",)], dma="cast", ring=True)
        S.barrier()
        layers()
        S.finalize()

        sems = {}
        def sem(key):
            if key not in sems:
                sems[key] = es.enter_context(nc.semaphore(f"s_{key}"))
            return sems[key]
        for e in COMPUTE:
            sem(e)
        final_tokens = {}
        for e in QUEUES:
            for op in S.q[e]:
                if op.is_dma:
                    final_tokens[op.dsem] = max(final_tokens.get(op.dsem, 0), op.dval)

        def replay(qname, eng, tail=False):
            waited = {}
            for op in S.q[qname]:
                need = {}
                for d in op.deps:
                    if d.is_dma:
                        k, v = d.dsem, d.dval
                    else:
                        k, v = d.eng, d.seq
                    if v > need.get(k, 0):
                        need[k] = v
                for k, v in need.items():
                    if waited.get(k, 0) < v:
                        eng.wait_ge(sem(k), v)
                        waited[k] = v
                ins = op.fn(eng)
                if op.is_dma:
                    ins.then_inc(sem(op.dsem), 16)
                elif op.needed:
                    ins.then_inc(sem(op.eng), 1)
            if tail:
                for k, v in final_tokens.items():
                    if waited.get(k, 0) < v:
                        eng.wait_ge(sem(k), v)

        block = es.enter_context(nc.Block())

        @block.tensor
        def _(e):
            replay("pe", e)

        @block.vector
        def _(e):
            replay("dve", e)

        @block.gpsimd
        def _(e):
            replay("pool", e)

        @block.sync
        def _(e):
            replay("sp", e)

        @block.scalar
        def _(e):
            replay("act", e, tail=True)

    return nc


def _run(cfg, inputs, nc=None):
    shared, cores = _host_prepare(cfg, inputs)
    if nc is None:
        nc = build_program(cfg)
    in_maps = []
    for c in range(cfg.NC):
        m = dict(shared)
        m.update(cores[c])
        in_maps.append(m)
    res = run_bass_kernel_spmd(nc, in_maps, core_ids=list(range(cfg.NC)))
    outs = [np.asarray(res.results[c]["out"]) for c in range(cfg.NC)]
    full = np.concatenate([o.T for o in outs], axis=0)
    return np.ascontiguousarray(full[None].astype(np.float32))


_PER_LAYER = ("p", "norm_mix", "w_in", "w_pool", "pool_scale", "q_norm", "k_norm", "rpb", "w_branch_pool",
              "w_branch_attn", "w_gate", "w_out", "norm_mlp", "w_up", "w_down", "norm_ple",
              "w_ple_gate_down", "w_ple_gate_up", "w_ple_proj")


def kernel_unfused(nc_cores=4, **inputs):
    depth = int(np.asarray(inputs["w_in"]).shape[0])
    cfg = Cfg(D=4096, L=1, NC=nc_cores, NR=256)
    nc = build_program(cfg)
    x = np.asarray(inputs["x"], dtype=np.float32)
    for l in range(depth):
        inp = {"x": x}
        for k in _PER_LAYER:
            inp[k] = np.asarray(inputs[k])[l:l + 1]
        x = _run(cfg, inp, nc=nc)
    return x


def kernel(**inputs):
    cfg = Cfg(D=4096, L=4, NC=8, NR=256)
    return _run(cfg, inputs)
```
